# Optimizing a Trainium2 kernel written in Bass

```python
import jax, jax.numpy as jnp
from jax import lax
import numpy as np


D_MODEL = 1024
BATCH = 16
SEQ = 2048
DEPTH = 4

CHUNK = 64
N_MIXERS = 2
N_SB_LAYERS = (DEPTH + 1) // 2
N_RG_LAYERS = DEPTH // 2

SB_HEADS = 16
SB_HEAD_DIM = D_MODEL // SB_HEADS
Q_BLOCK = 128

RG_WIDTH = D_MODEL
RG_BLOCKS = 16
RG_BLOCK_DIM = RG_WIDTH // RG_BLOCKS
RG_CONV = 4
RG_C = 8.0

D_FF = 4 * D_MODEL
EPS = 1e-6

kernel_name = 'hybrid_stickbreak_rglru_encoder'


def rms_norm(x, g):
    xf = x.astype(jnp.float32)
    y = xf * lax.rsqrt(jnp.mean(xf * xf, axis=-1, keepdims=True) + EPS)
    return (y * g.astype(jnp.float32)).astype(x.dtype)


def stick_breaking_attention(h, w_qkv, w_o):
    B, S, _ = h.shape
    qkv = (h @ w_qkv).reshape(B, S, 3, SB_HEADS, SB_HEAD_DIM)
    q = qkv[:, :, 0].transpose(0, 2, 1, 3)
    k = qkv[:, :, 1].transpose(0, 2, 1, 3)
    v = qkv[:, :, 2].transpose(0, 2, 1, 3)
    scale = SB_HEAD_DIM ** -0.5
    outs = []
    for start in range(0, S, Q_BLOCK):
        end = start + Q_BLOCK
        qb = q[:, :, start:end]
        kb = k[:, :, :end]
        vb = v[:, :, :end]
        z = jnp.einsum('bhqd,bhkd->bhqk', qb, kb).astype(jnp.float32) * scale
        q_pos = start + jnp.arange(Q_BLOCK)[:, None]
        k_pos = jnp.arange(end)[None, :]
        mask = k_pos < q_pos
        log_keep = jnp.where(mask, jax.nn.log_sigmoid(-z), 0.0)
        after = lax.cumsum(log_keep, axis=3, reverse=True) - log_keep
        w = jnp.where(mask, jnp.exp(jax.nn.log_sigmoid(z) + after), 0.0)
        outs.append(jnp.einsum('bhqk,bhkd->bhqd', w.astype(vb.dtype), vb))
    o = jnp.concatenate(outs, axis=2).transpose(0, 2, 1, 3).reshape(B, S, D_MODEL)
    return o @ w_o


def causal_depthwise_conv(x, w, b):
    S = x.shape[1]
    xp = jnp.pad(x, ((0, 0), (RG_CONV - 1, 0), (0, 0)))
    y = b
    for tap in range(RG_CONV):
        y = y + xp[:, tap:tap + S] * w[tap]
    return y


def lru_combine(e1, e2):
    a1, b1 = e1
    a2, b2 = e2
    return a1 * a2, a2 * b1 + b2


def rglru_block(h, w_in, conv_w, conv_b, w_a, b_a, w_x, b_x, lam, w_o):
    B, S, _ = h.shape
    proj = h @ w_in
    gate = jax.nn.gelu(proj[..., :RG_WIDTH])
    xc = causal_depthwise_conv(proj[..., RG_WIDTH:], conv_w, conv_b)
    xh = xc.reshape(B, S, RG_BLOCKS, RG_BLOCK_DIM)
    r = jax.nn.sigmoid(jnp.einsum('bshi,hij->bshj', xh, w_a).reshape(B, S, RG_WIDTH) + b_a)
    i = jax.nn.sigmoid(jnp.einsum('bshi,hij->bshj', xh, w_x).reshape(B, S, RG_WIDTH) + b_x)
    log_a = (-RG_C * r.astype(jnp.float32)) * jax.nn.softplus(-lam.astype(jnp.float32))
    a = jnp.exp(log_a)
    mult = jnp.sqrt(-jnp.expm1(2.0 * log_a))
    bterm = mult * (i * xc).astype(jnp.float32)
    _, hs = lax.associative_scan(lru_combine, (a, bterm), axis=1)
    y = hs.astype(h.dtype) * gate
    return y @ w_o


def squared_relu_mlp(h, w1, w2):
    u = jax.nn.relu(h @ w1)
    return (u * u) @ w2


def setup_inputs(seed: int = 0) -> dict:
    key = jax.random.key(seed)
    ks = jax.random.split(key, 20)
    f32 = jnp.float32
    res_scale = (2.0 * DEPTH) ** -0.5
    x = jax.random.normal(ks[0], (BATCH, SEQ, D_MODEL), f32)
    norm_mix = 1.0 + 0.02 * jax.random.normal(ks[1], (DEPTH, D_MODEL), f32)
    norm_mlp = 1.0 + 0.02 * jax.random.normal(ks[2], (DEPTH, D_MODEL), f32)
    mlp_w1 = jax.random.normal(ks[3], (DEPTH, D_MODEL, D_FF), f32) * D_MODEL ** -0.5
    mlp_w2 = jax.random.normal(ks[4], (DEPTH, D_FF, D_MODEL), f32) * (D_FF ** -0.5 * res_scale)
    sb_w_qkv = jax.random.normal(ks[5], (N_SB_LAYERS, D_MODEL, 3 * D_MODEL), f32) * D_MODEL ** -0.5
    sb_w_o = jax.random.normal(ks[6], (N_SB_LAYERS, D_MODEL, D_MODEL), f32) * (D_MODEL ** -0.5 * res_scale)
    rg_w_in = jax.random.normal(ks[7], (N_RG_LAYERS, D_MODEL, 2 * RG_WIDTH), f32) * D_MODEL ** -0.5
    rg_conv_w = jax.random.normal(ks[8], (N_RG_LAYERS, RG_CONV, RG_WIDTH), f32) * RG_CONV ** -0.5
    rg_conv_b = 0.01 * jax.random.normal(ks[9], (N_RG_LAYERS, RG_WIDTH), f32)
    rg_w_a = jax.random.normal(ks[10], (N_RG_LAYERS, RG_BLOCKS, RG_BLOCK_DIM, RG_BLOCK_DIM), f32) * RG_BLOCK_DIM ** -0.5
    rg_b_a = 0.01 * jax.random.normal(ks[11], (N_RG_LAYERS, RG_WIDTH), f32)
    rg_w_x = jax.random.normal(ks[12], (N_RG_LAYERS, RG_BLOCKS, RG_BLOCK_DIM, RG_BLOCK_DIM), f32) * RG_BLOCK_DIM ** -0.5
    rg_b_x = 0.01 * jax.random.normal(ks[13], (N_RG_LAYERS, RG_WIDTH), f32)
    u = jax.random.uniform(ks[14], (N_RG_LAYERS, RG_WIDTH), f32, 0.9, 0.999)
    s = u ** (1.0 / RG_C)
    rg_lambda = jnp.log(s) - jnp.log1p(-s)
    rg_w_o = jax.random.normal(ks[15], (N_RG_LAYERS, RG_WIDTH, D_MODEL), f32) * (RG_WIDTH ** -0.5 * res_scale)
    norm_final = 1.0 + 0.02 * jax.random.normal(ks[16], (D_MODEL,), f32)
    return {'x': x, 'norm_mix': norm_mix, 'norm_mlp': norm_mlp, 'mlp_w1': mlp_w1, 'mlp_w2': mlp_w2,
            'sb_w_qkv': sb_w_qkv, 'sb_w_o': sb_w_o, 'rg_w_in': rg_w_in, 'rg_conv_w': rg_conv_w,
            'rg_conv_b': rg_conv_b, 'rg_w_a': rg_w_a, 'rg_b_a': rg_b_a, 'rg_w_x': rg_w_x,
            'rg_b_x': rg_b_x, 'rg_lambda': rg_lambda, 'rg_w_o': rg_w_o, 'norm_final': norm_final}


def reference(x, norm_mix, norm_mlp, mlp_w1, mlp_w2, sb_w_qkv, sb_w_o, rg_w_in, rg_conv_w,
              rg_conv_b, rg_w_a, rg_b_a, rg_w_x, rg_b_x, rg_lambda, rg_w_o, norm_final):
    ia = 0
    ib = 0
    for layer in range(DEPTH):
        h = rms_norm(x, norm_mix[layer])
        if layer % N_MIXERS == 0:
            y = stick_breaking_attention(h, sb_w_qkv[ia], sb_w_o[ia])
            ia += 1
        else:
            y = rglru_block(h, rg_w_in[ib], rg_conv_w[ib], rg_conv_b[ib], rg_w_a[ib], rg_b_a[ib],
                            rg_w_x[ib], rg_b_x[ib], rg_lambda[ib], rg_w_o[ib])
            ib += 1
        x = x + y
        h = rms_norm(x, norm_mlp[layer])
        x = x + squared_relu_mlp(h, mlp_w1[layer], mlp_w2[layer])
    return rms_norm(x, norm_final)
```

```python
import contextlib
import numpy as np
import concourse.bass as bass
import concourse.mybir as mybir
from concourse.bass_utils import run_bass_kernel_spmd

F32 = mybir.dt.float32
BF16 = mybir.dt.bfloat16
AF = mybir.ActivationFunctionType
ALU = mybir.AluOpType

PE, ACT, DVE, POOL, SP = "tensor", "scalar", "vector", "gpsimd", "sync"
ENGS = (PE, ACT, DVE, POOL, SP)


class _Op:
    __slots__ = ("eng", "fn", "idx", "deps", "sig", "dma_sem", "dma_val", "waits", "sigval")

    def __init__(self, eng, fn):
        self.eng = eng
        self.fn = fn
        self.deps = set()
        self.sig = True
        self.dma_sem = None
        self.dma_val = 0
        self.waits = []
        self.sigval = 0


class Sched:
    def __init__(self, nc):
        self.nc = nc
        self.eng_ops = {e: [] for e in ENGS}
        self.track = {}
        self.dma_sems = {}
        self.stack = contextlib.ExitStack()
        self.eng_sem = {}
        for e in (PE, ACT, DVE, POOL):
            self.eng_sem[e] = self.stack.enter_context(nc.semaphore("s_" + e))

    def sbuf(self, name, shape, dtype):
        return self.stack.enter_context(self.nc.sbuf_tensor(name, list(shape), dtype))

    def psum(self, name, shape, dtype=F32):
        return self.stack.enter_context(self.nc.psum_tensor(name, list(shape), dtype))

    def dma_sem(self, name):
        if name not in self.dma_sems:
            self.dma_sems[name] = [self.stack.enter_context(self.nc.semaphore("d_" + name)), 0]
        return name

    def add(self, eng, fn, reads=(), writes=(), sig=True, dsem=None):
        op = _Op(eng, fn)
        op.sig = True
        lst = self.eng_ops[eng]
        op.idx = len(lst)
        lst.append(op)
        tr = self.track
        for k in reads:
            ent = tr.get(k)
            if ent is None:
                ent = tr[k] = [None, []]
            if ent[0] is not None:
                op.deps.add(ent[0])
            ent[1].append(op)
        for k in writes:
            ent = tr.get(k)
            if ent is None:
                ent = tr[k] = [None, []]
            if ent[0] is not None:
                op.deps.add(ent[0])
            for r in ent[1]:
                if r is not op:
                    op.deps.add(r)
            tr[k] = [op, []]
        if dsem is not None:
            self.dma_sem(dsem)
            ent = self.dma_sems[dsem]
            ent[1] += 16
            op.dma_sem = dsem
            op.dma_val = ent[1]
        return op

    def dma(self, queue, out, in_, reads, writes, dsem):
        return self.add(queue, lambda e: e.dma_start(out=out, in_=in_), reads, writes, dsem=dsem)

    def _completion(self, op):
        if op.dma_sem is not None:
            return ("d", op.dma_sem), op.dma_val
        assert op.eng in self.eng_sem, "dependency on non-signalling queue op"
        lst = self.eng_ops[op.eng]
        i = op.idx
        while not lst[i].sig:
            i += 1
        return ("e", op.eng), lst[i].sigval

    def finalize(self):
        for e in (PE, ACT, DVE, POOL):
            c = 0
            for op in self.eng_ops[e]:
                if op.dma_sem is None and op.sig:
                    c += 1
                    op.sigval = c
        nwaits = 0
        for e in ENGS:
            waited = {}
            for op in self.eng_ops[e]:
                need = {}
                for d in op.deps:
                    if e == PE and d.eng == PE:
                        continue
                    s, v = self._completion(d)
                    if s == ("e", e):
                        assert op.sig and v < op.sigval, "self-wait"
                    if v > need.get(s, 0):
                        need[s] = v
                for s, v in need.items():
                    if waited.get(s, 0) >= v:
                        continue
                    waited[s] = v
                    op.waits.append((s, v))
                    nwaits += 1
        return nwaits

    def _sem(self, s):
        return self.dma_sems[s[1]][0] if s[0] == "d" else self.eng_sem[s[1]]

    def emit(self):
        nc = self.nc
        self.finalize()
        with nc.Block() as block:
            for e in ENGS:
                ops = self.eng_ops[e]
                if not ops:
                    continue

                def body(eng, ops=ops, e=e):
                    for op in ops:
                        for s, v in op.waits:
                            eng.wait_ge(self._sem(s), v)
                        ins = op.fn(eng)
                        if op.dma_sem is not None:
                            ins.then_inc(self.dma_sems[op.dma_sem][0], 16)
                        elif op.sig and e in self.eng_sem and ins is not None:
                            ins.then_inc(self.eng_sem[e], 1)

                getattr(block, e)(body)

    def close(self):
        self.stack.close()


D = 1024
T = 2048
KC = 8
TW = 512
NT = T // TW
DEPTH = 4
DFF = 4096
FG = 512
NSEQ = 2
EPS = 1e-6
NEG = -30000.0

PC_MIX = 0
PC_MLP = 32
PC_FIN = 64
PC_RG = 72
NPRM = PC_RG + 128


def tsl(n):
    return slice(n * TW, (n + 1) * TW)


class Rot:
    def __init__(self, items):
        self.items = list(items)
        self.i = 0

    def next(self):
        it = self.items[self.i % len(self.items)]
        self.i += 1
        return it


class Ctx:
    pass


def build_program(layer_ids, do_final, nseq=NSEQ):
    nc = bass.Bass("TRN2", target_bir_lowering=False)
    S = Sched(nc)
    C = Ctx()
    C.nc, C.S = nc, S
    C.used = []

    def dt_in(name, shape, need=True):
        if not need:
            return None
        C.used.append(name)
        return nc.dram_tensor(name, list(shape), F32, kind="ExternalInput").ap()
    has_sb = any(l % 2 == 0 for l in layer_ids)
    has_rg = any(l % 2 == 1 for l in layer_ids)
    C.xin = dt_in("xT", [nseq, D, T])
    C.prm_d = dt_in("prm", [128, NPRM])
    C.csta_d = dt_in("csta", [128, 4 * 128])
    C.esel_d = dt_in("esel", [128, 16 * 128])
    C.ub_d = dt_in("ubneg", [128, 16 * 128])
    C.w1_d = dt_in("mlp_w1", [len(layer_ids), D, DFF])
    C.w2_d = dt_in("mlp_w2", [len(layer_ids), DFF, D])
    C.wqkv_d = dt_in("wqkv_r", [2, 8, D, 384], has_sb)
    C.wo_sb_d = dt_in("sb_w_o", [2, D, D], has_sb)
    C.win_d = dt_in("rg_w_in_r", [2, 8, D, 256], has_rg)
    C.wa_d = dt_in("rg_w_a", [2, 16, 64, 64], has_rg)
    C.wx_d = dt_in("rg_w_x", [2, 16, 64, 64], has_rg)
    C.wo_rg_d = dt_in("rg_w_o", [2, D, D], has_rg)
    C.yout = nc.dram_tensor("yT", [nseq, D, T], F32, kind="ExternalOutput").ap()

    C.x = S.sbuf("xres", [128, KC, T], F32)
    C.h = S.sbuf("hbuf", [128, KC, T], BF16)
    NSLOT = 3
    C.wslots = [S.sbuf("wslot%d" % i, [128, 4096], BF16) for i in range(NSLOT)]
    C.wrot = Rot(range(NSLOT))
    C.prm = S.sbuf("prm_sb", [128, NPRM], F32)
    C.csta = S.sbuf("csta_sb", [128, 4, 128], BF16)
    C.esel = S.sbuf("esel_sb", [128, 16, 128], BF16)
    C.epsc = S.sbuf("epsc", [128, 1], F32)
    C.wabd = S.sbuf("wabd", [128, KC, 128], BF16)
    C.wxbd = S.sbuf("wxbd", [128, KC, 128], BF16)
    C.c12 = S.sbuf("c12", [128, 3, KC], F32)
    C.carry = S.sbuf("carry", [128, KC], F32)
    C.sq = [S.sbuf("sq%d" % i, [128, TW], BF16) for i in range(2)]
    C.ntmp = S.sbuf("ntmp", [128, TW], F32)
    C.rstd = C.ntmp
    C.dummy = S.sbuf("mydummy", [128, 8], F32)
    ARENA = 71 * 1024 // 2
    C.arena = S.sbuf("arena", [128, ARENA], BF16)
    C.P = [S.psum("pb%d" % i, [128, TW]) for i in range(8)]

    ident = C.csta[:, 0, :]
    uneg = C.csta[:, 1, :]
    mneg = C.csta[:, 2, :]
    onesd = C.csta[:, 3, :]

    S.add(SP, lambda e: e.dma_start(out=C.prm[:], in_=C.prm_d), writes=["prm"], dsem="prm")
    S.add(POOL, lambda e: e.dma_start(out=C.csta[:], in_=C.csta_d.rearrange("p (a b) -> p a b", a=4)),
          writes=["csta"], dsem="csta")
    S.add(POOL, lambda e: e.dma_start(out=C.esel[:], in_=C.esel_d.rearrange("p (a b) -> p a b", a=16)),
          writes=["esel"], dsem="esel")
    S.add(DVE, lambda e: e.memset(C.epsc[:], EPS), writes=["epsc"])
    S.add(DVE, lambda e: e.memset(C.wabd[:], 0.0), writes=["wabd"])
    S.add(DVE, lambda e: e.memset(C.wxbd[:], 0.0), writes=["wxbd"])

    def barrier():
        S.add(DVE, lambda e: e.memset(C.dummy[:], 0.0), writes=["ARENA"])

    def pk(i):
        return ("P", i)

    def wfetch(src, shape, nm="w"):
        s = C.wrot.next()
        n = 1
        for d_ in shape[1:]:
            n *= d_
        flat = C.wslots[s][:, 0:n]
        if len(shape) == 3:
            view = flat.rearrange("p (a b) -> p a b", a=shape[1])
        else:
            view = flat
        key = ("w", s)
        S.add(POOL, lambda e: e.dma_start(out=view, in_=src), writes=[key], dsem="w%d" % s)
        return view, key

    def emit_norm(gcol, final=False):
        for n in range(NT):
            emit_norm_tile(gcol, n, final)

    def emit_norm_tile(gcol, n, final=False):
        if True:
            pst = 7
            for c in range(KC):
                sq = C.sq[c % 2]
                S.add(ACT, lambda e, sq=sq, c=c, n=n: e.activation(out=sq[:], in_=C.x[:, c, tsl(n)], func=AF.Square),
                      reads=[("x", c, n), "ARENA"], writes=[("sq", c % 2)])
                S.add(PE, lambda e, sq=sq, c=c: e.matmul(C.P[pst][:], lhsT=onesd, rhs=sq[:], start=(c == 0), stop=(c == KC - 1)),
                      reads=[("sq", c % 2), "csta", "ARENA"], writes=[pk(pst)], sig=(c == KC - 1))
            S.add(ACT, lambda e: e.activation(out=C.ntmp[:], in_=C.P[pst][:], func=AF.Ln, bias=C.epsc[:]),
                  reads=[pk(pst), "epsc", "ARENA"], writes=["ntmp", "rstd"])
            S.add(ACT, lambda e: e.activation(out=C.rstd[:], in_=C.ntmp[:], func=AF.Exp, scale=-0.5),
                  reads=["ntmp", "rstd", "ARENA"], writes=["ntmp", "rstd"])
            for c in range(KC):
                if final:
                    S.add(DVE, lambda e, c=c, n=n: e.scalar_tensor_tensor(
                        out=C.x[:, c, tsl(n)], in0=C.x[:, c, tsl(n)], scalar=C.prm[:, gcol + c:gcol + c + 1], in1=C.rstd[:],
                        op0=ALU.mult, op1=ALU.mult), reads=[("x", c, n), "rstd", "prm", "ARENA"], writes=[("x", c, n)])
                else:
                    S.add(DVE, lambda e, c=c, n=n: e.scalar_tensor_tensor(
                        out=C.h[:, c, tsl(n)], in0=C.x[:, c, tsl(n)], scalar=C.prm[:, gcol + c:gcol + c + 1], in1=C.rstd[:],
                        op0=ALU.mult, op1=ALU.mult), reads=[("x", c, n), "rstd", "prm", "ARENA"], writes=[("h", c, n)])

    def emit_mlp(l, post_tile=None):
        l = list(layer_ids).index(l)
        aT = [C.arena[:, i * 2048:(i + 1) * 2048].rearrange("p (a b) -> p a b", a=4) for i in range(2)]
        rt = [C.arena[:, 4096 + i * 1024: 4096 + (i + 1) * 1024].bitcast(F32) for i in range(3)]
        rotu = Rot([0, 1, 2])
        roty = Rot([3, 4, 5, 6])
        rotr = Rot(range(3))
        NG = DFF // FG
        wst = {}

        def U(g, n, it):
            if n == 0:
                wst[("w1", g)] = wfetch(C.w1_d[l][:, g * FG:(g + 1) * FG].rearrange("(c p) n -> p c n", p=128), [128, KC, FG])
            w1, k1 = wst[("w1", g)]
            a = aT[it % 2]
            ak = ("aT", it % 2)
            for fc in range(FG // 128):
                pu = rotu.next()
                for k in range(KC):
                    S.add(PE, lambda e, pu=pu, k=k, fc=fc: e.matmul(
                        C.P[pu][:], lhsT=w1[:, k, fc * 128:(fc + 1) * 128], rhs=C.h[:, k, tsl(n)],
                        start=(k == 0), stop=(k == KC - 1)),
                        reads=[k1, ("h", k, n), "ARENA"], writes=[pk(pu)])
                ri = rotr.next()
                S.add(ACT, lambda e, pu=pu, ri=ri: e.activation(out=rt[ri], in_=C.P[pu][:], func=AF.Relu),
                      reads=[pk(pu), "ARENA"], writes=[("rt", ri)])
                S.add(DVE, lambda e, ri=ri, fc=fc: e.tensor_tensor(out=a[:, fc, :], in0=rt[ri], in1=rt[ri], op=ALU.mult),
                      reads=[("rt", ri), "ARENA"], writes=[ak + (fc,)])

        def Y(g, n, it):
            if n == 0:
                wst[("w2", g)] = wfetch(C.w2_d[l][g * FG:(g + 1) * FG, :].rearrange("(c p) n -> p c n", p=128), [128, FG // 128, D])
            w2, k2 = wst[("w2", g)]
            a = aT[it % 2]
            ak = ("aT", it % 2)
            for dc in range(KC):
                py = roty.next()
                for fc in range(FG // 128):
                    S.add(PE, lambda e, py=py, fc=fc, dc=dc: e.matmul(
                        C.P[py][:], lhsT=w2[:, fc, dc * 128:(dc + 1) * 128], rhs=a[:, fc, :],
                        start=(fc == 0), stop=(fc == FG // 128 - 1)),
                        reads=[k2, ak + (fc,), "ARENA"], writes=[pk(py)])
                S.add(DVE, lambda e, py=py, dc=dc: e.tensor_tensor(
                    out=C.x[:, dc, tsl(n)], in0=C.P[py][:], in1=C.x[:, dc, tsl(n)], op=ALU.add),
                    reads=[pk(py), ("x", dc, n), "ARENA"], writes=[("x", dc, n)])
            if g == NG - 1 and post_tile is not None:
                if n >= 1:
                    post_tile(n - 1)
                if n == NT - 1:
                    post_tile(n)

        its = [(g, n) for g in range(NG) for n in range(NT)]
        U(its[0][0], its[0][1], 0)
        for it, (g, n) in enumerate(its):
            if it + 1 < len(its):
                U(its[it + 1][0], its[it + 1][1], it + 1)
            Y(g, n, it)

    def emit_attn(ia, post_tile=None):
        A = C.arena
        o = [0]

        def carve(nelem):
            v = A[:, o[0]:o[0] + nelem]
            o[0] += nelem
            return v
        qz = [[carve(T) for _ in range(2)] for _ in range(2)]
        kz = [[carve(T) for _ in range(2)] for _ in range(2)]
        vc = [carve(T).rearrange("p (a b) -> p a b", a=16) for _ in range(2)]
        oT1 = carve(T)
        oT = [oT1, oT1]
        lp = [carve(16 * TW).rearrange("p (a b) -> p a b", a=16), carve(8 * TW).rearrange("p (a b) -> p a b", a=8)]
        wsb = [carve(TW) for _ in range(3)]
        assert o[0] <= A.shape[1], o[0]
        rotS = Rot([0, 1])
        rotA = Rot([2, 3])
        PTOT = 4
        PO = 5
        rotP = Rot([6, 7])
        rotw = Rot(range(3))
        for sp in range(2):
            for hh in range(2):
                ob = 64 * (1 - hh)
                S.add(POOL, lambda e, sp=sp, hh=hh, ob=ob: e.dma_start(out=kz[sp][hh][ob:ob + 64, :], in_=C.ub_d[ob:ob + 64, :]),
                      reads=["ARENA"], writes=[("kzp", sp, hh)], dsem="kzp%d%d" % (sp, hh))

        def unit_steps(c, hh, Q, ls):
            sp = c % 2
            ob = 64 * (1 - hh)
            nb = 4 * Q + 4
            geo = []
            for b in range(nb):
                t0 = max(Q * TW, b * 128)
                geo.append((t0, (Q + 1) * TW - t0, t0 - Q * TW, b >= 4 * Q))
            st1 = {}
            st2 = {}
            qk_reads = lambda b: [("kz", sp, hh, b // 4), ("kzp", sp, hh), ("qz", sp, hh, Q), ("qzt", sp, hh, Q), "ARENA"]

            def A1(b):
                t0, W_, off, diag = geo[b]
                ps = rotS.next()
                st1[b] = ps
                S.add(PE, lambda e: e.matmul(C.P[ps][:, 0:W_], lhsT=kz[sp][hh][:, b * 128:(b + 1) * 128],
                                             rhs=qz[sp][hh][:, t0:t0 + W_], start=True, stop=(not diag)),
                      reads=qk_reads(b), writes=[pk(ps)])
                if diag:
                    S.add(PE, lambda e: e.matmul(C.P[ps][:, 0:128], lhsT=ident, rhs=mneg, start=False, stop=True),
                          reads=["csta", "ARENA"], writes=[pk(ps)])

            def B1(b):
                t0, W_, off, diag = geo[b]
                ps = st1[b]
                S.add(ACT, lambda e: e.activation(out=C.P[ps][:, 0:W_], in_=C.P[ps][:, 0:W_], func=AF.Exp),
                      reads=[pk(ps), "ARENA"], writes=[pk(ps)])
                S.add(ACT, lambda e: e.activation(out=lp[ls][:, b, 0:W_], in_=C.P[ps][:, 0:W_], func=AF.Ln, bias=1.0),
                      reads=[pk(ps), "ARENA"], writes=[("lp", ls, b)])

            def C1(b):
                t0, W_, off, diag = geo[b]
                S.add(PE, lambda e: e.matmul(C.P[PTOT][:, off:off + W_], lhsT=C.esel[:, b, :], rhs=lp[ls][:, b, 0:W_],
                                             start=(b == 0), stop=(b == nb - 1)),
                      reads=[("lp", ls, b), "esel", "ARENA"], writes=[pk(PTOT)])

            def F1():
                S.add(DVE, lambda e: e.tensor_copy(out=qz[sp][hh][ob:ob + 48, tsl(Q)], in_=C.P[PTOT][ob:ob + 48, :]),
                      reads=[pk(PTOT), "ARENA"], writes=[("qzt", sp, hh, Q)])
                S.add(DVE, lambda e: e.tensor_tensor(out=qz[sp][hh][ob + 32:ob + 48, tsl(Q)], in0=C.P[PTOT][ob + 32:ob + 48, :],
                                                     in1=qz[sp][hh][ob + 32:ob + 48, tsl(Q)], op=ALU.subtract),
                      reads=[pk(PTOT), ("qzt", sp, hh, Q), "ARENA"], writes=[("qzt", sp, hh, Q)])

            def A2(b):
                t0, W_, off, diag = geo[b]
                pa = rotA.next()
                st2[b] = pa
                S.add(PE, lambda e: e.matmul(C.P[pa][:, 0:W_], lhsT=uneg, rhs=lp[ls][:, b, 0:W_], start=True, stop=False),
                      reads=[("lp", ls, b), "csta", "ARENA"], writes=[pk(pa)])
                S.add(PE, lambda e: e.matmul(C.P[pa][:, 0:W_], lhsT=kz[sp][hh][:, b * 128:(b + 1) * 128],
                                             rhs=qz[sp][hh][:, t0:t0 + W_], start=False, stop=(not diag)),
                      reads=qk_reads(b), writes=[pk(pa)])
                if diag:
                    S.add(PE, lambda e: e.matmul(C.P[pa][:, 0:128], lhsT=ident, rhs=mneg, start=False, stop=True),
                          reads=["csta", "ARENA"], writes=[pk(pa)])

            def B2(b):
                t0, W_, off, diag = geo[b]
                pa = st2[b]
                wi = rotw.next()
                st2[("w", b)] = wi
                S.add(ACT, lambda e: e.activation(out=wsb[wi][:, 0:W_], in_=C.P[pa][:, 0:W_], func=AF.Exp),
                      reads=[pk(pa), "ARENA"], writes=[("wsb", wi)])

            def C2(b):
                t0, W_, off, diag = geo[b]
                wi = st2[("w", b)]
                S.add(PE, lambda e: e.matmul(C.P[PO][:, off:off + W_], lhsT=vc[sp][:, b, :], rhs=wsb[wi][:, 0:W_],
                                             start=(b == 0), stop=(b == nb - 1)),
                      reads=[("wsb", wi), ("v", sp, b // 4), "ARENA"], writes=[pk(PO)])

            def F2():
                po = 64 * hh
                S.add(DVE, lambda e: e.tensor_copy(out=oT[sp][po:po + 64, tsl(Q)], in_=C.P[PO][po:po + 64, :]),
                      reads=[pk(PO), "ARENA"], writes=[("oT", Q, hh)])

            def pipe(Af, Bf, Cf, Ff):
                steps = [[lambda: Af(0)]]
                for b in range(nb):
                    st = []
                    if b + 1 < nb:
                        st.append(lambda b=b: Af(b + 1))
                    st.append(lambda b=b: Bf(b))
                    if b >= 1:
                        st.append(lambda b=b: Cf(b - 1))
                    steps.append(st)
                steps.append([lambda: Cf(nb - 1), Ff])
                return steps
            return pipe(A1, B1, C1, F1), pipe(A2, B2, C2, F2)

        def run(steps):
            for st in steps:
                for f in st:
                    f()

        def mergeL(s1, s2):
            n1, n2 = len(s1), len(s2)
            i1 = i2 = 0
            out = []
            while i1 < n1 or i2 < n2:
                if i2 >= n2 or (i1 < n1 and i1 * n2 <= i2 * n1):
                    out.append(s1[i1])
                    i1 += 1
                else:
                    out.append(s2[i2])
                    i2 += 1
            return out

        def proj_steps(c):
            sp = c % 2
            st = {}

            def fetch():
                st["w"] = wfetch(C.wqkv_d[ia][c].rearrange("(c p) n -> p c n", p=128), [128, KC, 384])
                for hh in range(2):
                    ob = 64 * (1 - hh)
                    S.add(DVE, lambda e, hh=hh, ob=ob: e.memset(qz[sp][hh][ob:ob + 64, :], 0.0), reads=["ARENA"],
                          writes=[("qzt", sp, hh, Q) for Q in range(NT)])
            steps = [[fetch]]
            for n in range(NT):
                for which in range(2):
                    for k in range(KC):
                        def f(n=n, which=which, k=k):
                            wq, kq = st["w"]
                            if k == 0:
                                st["pp"] = rotP.next()
                            pp = st["pp"]
                            S.add(PE, lambda e: e.matmul(
                                C.P[pp][:], lhsT=wq[:, k, which * 128:(which + 1) * 128], rhs=C.h[:, k, tsl(n)],
                                start=(k == 0), stop=(k == KC - 1)),
                                reads=[kq, ("h", k, n), "ARENA"], writes=[pk(pp)])
                            if k == KC - 1:
                                for hh in range(2):
                                    if which == 0:
                                        S.add(DVE, lambda e, hh=hh: e.tensor_scalar(
                                            out=qz[sp][hh][64 * hh:64 * hh + 64, tsl(n)], in0=C.P[pp][64 * hh:64 * hh + 64, :],
                                            scalar1=0.125, scalar2=None, op0=ALU.mult),
                                            reads=[pk(pp), "ARENA"], writes=[("qz", sp, hh, n)])
                                    else:
                                        S.add(DVE, lambda e, hh=hh: e.tensor_copy(
                                            out=kz[sp][hh][64 * hh:64 * hh + 64, tsl(n)], in_=C.P[pp][64 * hh:64 * hh + 64, :]),
                                            reads=[pk(pp), "ARENA"], writes=[("kz", sp, hh, n)])
                        steps.append([f])
            for t4 in range(4):
                for j in range(4):
                    for k2 in range(KC // 2):
                        def f(t4=t4, j=j, k2=k2):
                            wq, kq = st["w"]
                            if j == 0 and k2 == 0:
                                st["pv"] = rotP.next()
                            pp = st["pv"]
                            tb = t4 * 4 + j
                            for k in (2 * k2, 2 * k2 + 1):
                                S.add(PE, lambda e, k=k: e.matmul(
                                    C.P[pp][:, j * 128:(j + 1) * 128], lhsT=C.h[:, k, tb * 128:(tb + 1) * 128], rhs=wq[:, k, 256:384],
                                    start=(k == 0), stop=(k == KC - 1)),
                                    reads=[kq, ("h", k, t4), "ARENA"], writes=[pk(pp)])
                            if j == 3 and k2 == KC // 2 - 1:
                                S.add(DVE, lambda e: e.tensor_copy(
                                    out=vc[sp][:, t4 * 4:(t4 + 1) * 4, :], in_=C.P[pp][:].rearrange("p (a b) -> p a b", a=4)),
                                    reads=[pk(pp), "ARENA"], writes=[("v", sp, t4)])
                        steps.append([f])
            return steps

        def wo_steps(c):
            sp = c % 2
            st = {}

            def fetch():
                st["w"] = wfetch(C.wo_sb_d[ia][c * 128:(c + 1) * 128, :], [128, D])
            steps = [[fetch]]
            for n in range(NT):
                for dc in range(KC):
                    def f(n=n, dc=dc):
                        wo, ko = st["w"]
                        pp = rotP.next()
                        S.add(PE, lambda e: e.matmul(
                            C.P[pp][:], lhsT=wo[:, dc * 128:(dc + 1) * 128], rhs=oT[sp][:, tsl(n)], start=True, stop=True),
                            reads=[ko, ("oT", n, 0), ("oT", n, 1), "ARENA"], writes=[pk(pp)])
                        S.add(DVE, lambda e: e.tensor_tensor(
                            out=C.x[:, dc, tsl(n)], in0=C.P[pp][:], in1=C.x[:, dc, tsl(n)], op=ALU.add),
                            reads=[pk(pp), ("x", dc, n), "ARENA"], writes=[("x", dc, n)])
                    steps.append([f])
            return steps

        NP = 8
        order = ((3, 0), (1, 0), (3, 1), (1, 1), (2, 0), (0, 0), (2, 1), (0, 1))
        units = []
        for c in range(NP):
            for i, (Q, hh) in enumerate(order):
                units.append(unit_steps(c, hh, Q, i % 2))
        run(proj_steps(0))
        run(units[0][0])
        side = []
        for k in range(len(units)):
            c, i = divmod(k, 8)
            blk = units[k][1]
            tail = []
            if k + 1 < len(units):
                hb = 3 if len(blk) > 5 else 1
                tail = blk[-hb:]
                blk = mergeL(blk[:-hb], units[k + 1][0])
            if i == 0:
                side = []
                if c >= 1:
                    blk = mergeL(blk, wo_steps(c - 1))
                if c + 1 < NP:
                    side = proj_steps(c + 1)
            i0 = 1 if c >= 1 else 0
            if side and i0 <= i < 7:
                kk = (len(side) + (6 - i)) // (7 - i)
                blk = mergeL(blk, side[:kk])
                side = side[kk:]
            run(blk + tail)
        last = wo_steps(NP - 1)
        run(last[:1])
        for n in range(NT):
            run(last[1 + n * KC:1 + (n + 1) * KC])
            if post_tile is not None:
                if n >= 1:
                    post_tile(n - 1)
                if n == NT - 1:
                    post_tile(n)

    def emit_rg(ib, post_tile=None):
        A = C.arena
        o = [0]

        def carve32(nelem):
            v = A[:, o[0]:o[0] + 2 * nelem].bitcast(F32)
            o[0] += 2 * nelem
            return v

        def carve16(nelem):
            v = A[:, o[0]:o[0] + nelem]
            o[0] += nelem
            return v
        T2 = T // 2
        xb = carve32(T + 4)
        SL = [dict(xc=carve32(T2), rr=carve32(T2), ii=carve32(T2), mm=carve32(T2), gate=carve32(T2)) for _ in range(2)]
        xcb = carve16(T2)
        yb = [carve16(2 * T).rearrange("p (a b) -> p a b", a=2) for _ in range(2)]
        assert o[0] <= A.shape[1], o[0]
        pc = PC_RG + ib * 64
        rotP = Rot(range(8))
        for two in range(2):
            for (dst, src, nm) in ((C.wabd, C.wa_d, "wabd"), (C.wxbd, C.wx_d, "wxbd")):
                S.add(POOL, lambda e, two=two, dst=dst, src=src: e.dma_start(
                    out=dst[two * 64:(two + 1) * 64, :, two * 64:(two + 1) * 64],
                    in_=src[ib].rearrange("(c two) i j -> two i c j", two=2)[two]),
                    writes=[nm], dsem=nm)
        S.add(ACT, lambda e: e.activation(out=C.c12[:, 0, :], in_=C.prm[:, pc + 56:pc + 64], func=AF.Exp, scale=-1.0),
              reads=["prm", "ARENA"], writes=["c12"])
        S.add(ACT, lambda e: e.activation(out=C.c12[:, 0, :], in_=C.c12[:, 0, :], func=AF.Ln, bias=1.0),
              reads=["c12", "ARENA"], writes=["c12"])
        S.add(DVE, lambda e: e.tensor_scalar(out=C.c12[:, 1, :], in0=C.c12[:, 0, :], scalar1=-8.0, scalar2=None, op0=ALU.mult),
              reads=["c12", "ARENA"], writes=["c12"])
        S.add(DVE, lambda e: e.tensor_scalar(out=C.c12[:, 2, :], in0=C.c12[:, 0, :], scalar1=-16.0, scalar2=None, op0=ALU.mult),
              reads=["c12", "ARENA"], writes=["c12"])
        S.add(DVE, lambda e: e.memset(xb[:, 0:4], 0.0), reads=["ARENA"], writes=["xbpad"])
        wgs = {}

        def item_steps(c, hf, slot):
            B_ = SL[slot]
            xc, rr, ii, mm_, gate = B_["xc"], B_["rr"], B_["ii"], B_["mm"], B_["gate"]
            tiles = (2 * hf, 2 * hf + 1)
            sk = lambda nm: (nm, slot)
            ps_ = (c // 2) % 2

            def f_xproj():
                if hf == 0:
                    wgs[c] = wfetch(C.win_d[ib][c].rearrange("(c p) n -> p c n", p=128), [128, KC, 256])
                wg, kg = wgs[c]
                for n in tiles:
                    pp = rotP.next()
                    for k in range(KC):
                        S.add(PE, lambda e, pp=pp, k=k, n=n: e.matmul(
                            C.P[pp][:], lhsT=wg[:, k, 128:256], rhs=C.h[:, k, tsl(n)], start=(k == 0), stop=(k == KC - 1)),
                            reads=[kg, ("h", k, n), "ARENA"], writes=[pk(pp)])
                    S.add(ACT, lambda e, pp=pp, n=n: e.activation(out=xb[:, 4 + n * TW:4 + (n + 1) * TW], in_=C.P[pp][:], func=AF.Copy),
                          reads=[pk(pp), "ARENA"], writes=[("xb", n)])

            def f_conv():
                xbk = [("xb", n) for n in ((0, 1) if hf == 0 else (1, 2, 3))] + ["xbpad"]
                base = 1 + hf * T2
                S.add(DVE, lambda e: e.tensor_scalar(
                    out=xc[:], in0=xb[:, base:base + T2], scalar1=C.prm[:, pc + c:pc + c + 1],
                    scalar2=C.prm[:, pc + 32 + c:pc + 32 + c + 1], op0=ALU.mult, op1=ALU.add),
                    reads=xbk + ["prm", "ARENA"], writes=[sk("xc")])
                for tap in range(1, 4):
                    S.add(DVE, lambda e, tap=tap: e.scalar_tensor_tensor(
                        out=xc[:], in0=xb[:, base + tap:base + tap + T2], scalar=C.prm[:, pc + tap * 8 + c:pc + tap * 8 + c + 1],
                        in1=xc[:], op0=ALU.mult, op1=ALU.add), reads=xbk + [sk("xc"), "prm", "ARENA"], writes=[sk("xc")])
                S.add(ACT, lambda e: e.activation(out=xcb[:], in_=xc[:], func=AF.Copy), reads=[sk("xc"), "ARENA"], writes=["xcb"])

            def f_gates():
                for (wbd, wkey, dst, dkey, bcol) in ((C.wabd, "wabd", rr, "rr", pc + 40), (C.wxbd, "wxbd", ii, "ii", pc + 48)):
                    for j in range(2):
                        pp = rotP.next()
                        S.add(PE, lambda e, pp=pp, j=j, wbd=wbd: e.matmul(
                            C.P[pp][:], lhsT=wbd[:, c, :], rhs=xcb[:, j * TW:(j + 1) * TW], start=True, stop=True),
                            reads=[wkey, "xcb", "ARENA"], writes=[pk(pp)])
                        S.add(ACT, lambda e, pp=pp, j=j, dst=dst, bcol=bcol: e.activation(
                            out=dst[:, j * TW:(j + 1) * TW], in_=C.P[pp][:], func=AF.Sigmoid, bias=C.prm[:, bcol + c:bcol + c + 1]),
                            reads=[pk(pp), "prm", "ARENA"], writes=[sk(dkey)])

            def f_exps():
                S.add(ACT, lambda e: e.activation(out=mm_[:], in_=rr[:], func=AF.Exp, scale=C.c12[:, 2, c:c + 1]),
                      reads=[sk("rr"), "c12", "ARENA"], writes=[sk("mm")])
                S.add(ACT, lambda e: e.activation(out=rr[:], in_=rr[:], func=AF.Exp, scale=C.c12[:, 1, c:c + 1]),
                      reads=[sk("rr"), "c12", "ARENA"], writes=[sk("rr")])
                S.add(ACT, lambda e: e.activation(out=mm_[:], in_=mm_[:], func=AF.Sqrt, scale=-1.0, bias=1.0),
                      reads=[sk("mm"), "ARENA"], writes=[sk("mm")])

            def f_gateproj():
                wg, kg = wgs[c]
                for j, n in enumerate(tiles):
                    pp = rotP.next()
                    for k in range(KC):
                        S.add(PE, lambda e, pp=pp, k=k, n=n: e.matmul(
                            C.P[pp][:], lhsT=wg[:, k, 0:128], rhs=C.h[:, k, tsl(n)], start=(k == 0), stop=(k == KC - 1)),
                            reads=[kg, ("h", k, n), "ARENA"], writes=[pk(pp)])
                    S.add(ACT, lambda e, pp=pp, j=j: e.activation(out=gate[:, j * TW:(j + 1) * TW], in_=C.P[pp][:], func=AF.Gelu_apprx_tanh),
                          reads=[pk(pp), "ARENA"], writes=[sk("gate")])

            def b_mul():
                S.add(DVE, lambda e: e.tensor_tensor(out=ii[:], in0=ii[:], in1=xc[:], op=ALU.mult),
                      reads=[sk("ii"), sk("xc"), "ARENA"], writes=[sk("ii")])
                S.add(DVE, lambda e: e.tensor_tensor(out=ii[:], in0=ii[:], in1=mm_[:], op=ALU.mult),
                      reads=[sk("ii"), sk("mm"), "ARENA"], writes=[sk("ii")])

            def b_scan():
                if hf == 0:
                    S.add(DVE, lambda e: e.tensor_tensor_scan(out=xc[:], data0=rr[:], data1=ii[:], initial=0.0,
                                                               op0=ALU.mult, op1=ALU.add),
                          reads=[sk("rr"), sk("ii"), "ARENA"], writes=[sk("xc")])
                    S.add(DVE, lambda e: e.tensor_copy(out=C.carry[:, c:c + 1], in_=xc[:, T2 - 1:T2]),
                          reads=[sk("xc"), "ARENA"], writes=[("carry", c)])
                else:
                    S.add(DVE, lambda e: e.tensor_tensor_scan(out=xc[:], data0=rr[:], data1=ii[:], initial=C.carry[:, c:c + 1],
                                                               op0=ALU.mult, op1=ALU.add),
                          reads=[sk("rr"), sk("ii"), ("carry", c), "ARENA"], writes=[sk("xc")])

            def b_y():
                S.add(DVE, lambda e: e.tensor_tensor(out=yb[ps_][:, c % 2, hf * T2:(hf + 1) * T2], in0=xc[:], in1=gate[:], op=ALU.mult),
                      reads=[sk("xc"), sk("gate"), "ARENA"], writes=[("yb", ps_, c % 2, hf)])
            return [[f_xproj], [f_conv], [f_gates], [f_exps], [f_gateproj]], [[b_mul], [b_scan], [b_y]]

        def wo_steps(m):
            ps_ = m % 2
            st = {}

            def fetch():
                st["w"] = wfetch(C.wo_rg_d[ib][m * 256:(m + 1) * 256, :].rearrange("(c p) n -> p c n", p=128), [128, 2, D])
            steps = [[fetch]]
            for n in range(NT):
                for dc in range(KC):
                    def f(n=n, dc=dc):
                        wo, ko = st["w"]
                        pp = rotP.next()
                        for cc in range(2):
                            S.add(PE, lambda e, pp=pp, cc=cc: e.matmul(
                                C.P[pp][:], lhsT=wo[:, cc, dc * 128:(dc + 1) * 128], rhs=yb[ps_][:, cc, tsl(n)],
                                start=(cc == 0), stop=(cc == 1)),
                                reads=[ko, ("yb", ps_, cc, n // 2), "ARENA"], writes=[pk(pp)])
                        S.add(DVE, lambda e, pp=pp: e.tensor_tensor(
                            out=C.x[:, dc, tsl(n)], in0=C.P[pp][:], in1=C.x[:, dc, tsl(n)], op=ALU.add),
                            reads=[pk(pp), ("x", dc, n), "ARENA"], writes=[("x", dc, n)])
                    steps.append([f])
            return steps

        def run(steps):
            for st in steps:
                for f in st:
                    f()

        def mergeL(s1, s2):
            n1, n2 = len(s1), len(s2)
            i1 = i2 = 0
            out = []
            while i1 < n1 or i2 < n2:
                if i2 >= n2 or (i1 < n1 and i1 * n2 <= i2 * n1):
                    out.append(s1[i1])
                    i1 += 1
                else:
                    out.append(s2[i2])
                    i2 += 1
            return out

        items = [(c, hf) for c in range(KC) for hf in range(2)]
        FB = [item_steps(c, hf, i % 2) for i, (c, hf) in enumerate(items)]
        run(FB[0][0])
        wo_pending = []
        for i in range(len(items)):
            blk = FB[i][1]
            if i + 1 < len(items):
                blk = mergeL(blk, FB[i + 1][0])
            if i % 4 == 0 and i >= 4:
                wo_pending = wo_steps(i // 4 - 1)
            if wo_pending:
                k = (len(wo_pending) + (3 - i % 4)) // (4 - i % 4)
                blk = mergeL(blk, wo_pending[:k])
                wo_pending = wo_pending[k:]
            run(blk)
        last = wo_steps(KC // 2 - 1)
        run(last[:1])
        for n in range(NT):
            run(last[1 + n * KC:1 + (n + 1) * KC])
            if post_tile is not None:
                if n >= 1:
                    post_tile(n - 1)
                if n == NT - 1:
                    post_tile(n)

    allx = [("x", c, n) for c in range(KC) for n in range(NT)]
    for s in range(nseq):
        for n in range(NT):
            for c in range(KC):
                S.add(SP, lambda e, s=s, c=c, n=n: e.dma_start(out=C.x[:, c, tsl(n)], in_=C.xin[s, c * 128:(c + 1) * 128, tsl(n)]),
                      writes=[("x", c, n)], dsem="xin%d_%d" % (c, n))
        emit_norm(PC_MIX + layer_ids[0] * 8)
        for li, l in enumerate(layer_ids):
            barrier()
            mlp_norm = lambda n, l=l: emit_norm_tile(PC_MLP + l * 8, n)
            if l % 2 == 0:
                emit_attn(l // 2, post_tile=mlp_norm)
            else:
                emit_rg(l // 2, post_tile=mlp_norm)
            barrier()
            if li + 1 < len(layer_ids):
                nxt = lambda n, l2=layer_ids[li + 1]: emit_norm_tile(PC_MIX + l2 * 8, n)
            elif do_final:
                nxt = lambda n: emit_norm_tile(PC_FIN, n, True)
            else:
                nxt = None
            emit_mlp(l, post_tile=nxt)
        for n in range(NT):
            for c in range(KC):
                S.add(SP, lambda e, s=s, c=c, n=n: e.dma_start(out=C.yout[s, c * 128:(c + 1) * 128, tsl(n)], in_=C.x[:, c, tsl(n)]),
                      reads=[("x", c, n)], writes=[("yout", s, c, n)], dsem="yout%d_%d" % (c, n))
    S.add(SP, lambda e: None, reads=[("yout", s, c, n) for s in range(nseq) for c in range(KC) for n in range(NT)])
    S.emit()
    S.close()
    nc.used_inputs = list(C.used)
    return nc


def _consts():
    j = np.arange(128)
    ident = np.eye(128, dtype=np.float32)
    uneg = -(j[:, None] >= j[None, :]).astype(np.float32)
    mneg = np.where(j[:, None] >= j[None, :], NEG, 0.0).astype(np.float32)
    onesd = np.full((128, 128), 1.0 / D, np.float32)
    csta = np.concatenate([ident, uneg, mneg, onesd], axis=1)
    esel = np.zeros((128, 16, 128), np.float32)
    ub = np.zeros((128, 16, 128), np.float32)
    for b in range(16):
        for base in (0, 32, 64, 96):
            esel[:, b, base + b] = 1.0
            ub[base + b + 1:base + 16, b, :] = -1.0
    return csta, esel.reshape(128, 2048), ub.reshape(128, 2048)


def _pack_params(inp):
    prm = np.zeros((128, NPRM), np.float32)

    def put(col, vec):
        prm[:, col:col + 8] = np.asarray(vec, np.float32).reshape(8, 128).T
    for l in range(DEPTH):
        put(PC_MIX + l * 8, inp["norm_mix"][l])
        put(PC_MLP + l * 8, inp["norm_mlp"][l])
    put(PC_FIN, inp["norm_final"])
    for ib in range(2):
        pc = PC_RG + ib * 64
        for tap in range(4):
            put(pc + tap * 8, inp["rg_conv_w"][ib][tap])
        put(pc + 32, inp["rg_conv_b"][ib])
        put(pc + 40, inp["rg_b_a"][ib])
        put(pc + 48, inp["rg_b_x"][ib])
        put(pc + 56, inp["rg_lambda"][ib])
    return prm


_PROG_CACHE = {}


def _get_prog(layer_ids, do_final):
    key = (tuple(layer_ids), do_final)
    if key not in _PROG_CACHE:
        _PROG_CACHE[key] = build_program(list(layer_ids), do_final)
    return _PROG_CACHE[key]


def _host_layout(inp):
    f = lambda a: np.ascontiguousarray(np.asarray(a, dtype=np.float32))
    wqkv = np.asarray(inp["sb_w_qkv"], np.float32).reshape(2, D, 3, 8, 128)
    wqkv_r = np.ascontiguousarray(wqkv.transpose(0, 3, 1, 2, 4).reshape(2, 8, D, 384))
    win = np.asarray(inp["rg_w_in"], np.float32).reshape(2, D, 2, 8, 128)
    win_r = np.ascontiguousarray(win.transpose(0, 3, 1, 2, 4).reshape(2, 8, D, 256))
    csta, esel, ub = _consts()
    shared = {
        "prm": _pack_params(inp), "csta": csta, "esel": esel, "ubneg": ub,
        "mlp_w1": f(inp["mlp_w1"]), "mlp_w2": f(inp["mlp_w2"]), "wqkv_r": wqkv_r, "sb_w_o": f(inp["sb_w_o"]),
        "rg_w_in_r": win_r, "rg_w_a": f(inp["rg_w_a"]), "rg_w_x": f(inp["rg_w_x"]), "rg_w_o": f(inp["rg_w_o"]),
    }
    return shared


FUSED = True


def kernel(**inp):
    x = np.asarray(inp["x"], np.float32)
    B = x.shape[0]
    ncores = 8
    shared = _host_layout(inp)
    xT = np.ascontiguousarray(x.transpose(0, 2, 1)).reshape(ncores, NSEQ, D, T)
    stages = [((0, 1, 2, 3), True)] if FUSED else [((0,), False), ((1,), False), ((2,), False), ((3,), True)]
    cur = xT
    for layer_ids, fin in stages:
        nc = _get_prog(layer_ids, fin)
        in_maps = []
        for i in range(ncores):
            m = {k: shared[k] for k in nc.used_inputs if k != "xT"}
            if len(layer_ids) < DEPTH:
                m["mlp_w1"] = np.ascontiguousarray(shared["mlp_w1"][list(layer_ids)])
                m["mlp_w2"] = np.ascontiguousarray(shared["mlp_w2"][list(layer_ids)])
            m["xT"] = cur[i]
            in_maps.append(m)
        res = run_bass_kernel_spmd(nc, in_maps, core_ids=list(range(ncores)))
        cur = np.stack([r["yT"] for r in res.results], axis=0)
    out = cur.reshape(B, D, T).transpose(0, 2, 1)
    return np.ascontiguousarray(out).astype(np.float32)
```
